# Optimizing a Trainium2 kernel written in Bass

```python
import jax, jax.numpy as jnp
from jax import lax
import numpy as np

D_MODEL = 1024
BATCH = 2
SEQ = 8192
DEPTH = 1

N_META = 16
HGRN_HEADS = 8
HGRN_EXPAND = 128
HGRN_FDIM = HGRN_HEADS * HGRN_EXPAND
HGRN_VDIM = D_MODEL
HGRN_HEAD_V = HGRN_VDIM // HGRN_HEADS
CONV_DIM = D_MODEL
CONV_WIDTH = 3
D_FF = 2816
FFN_CONV_WIDTH = 3
CHUNK = 64
EPS = 1e-6
IN_SIZES = (HGRN_FDIM,
            HGRN_FDIM,
            HGRN_VDIM,
            HGRN_VDIM,
            CONV_DIM,
            CONV_DIM,
            CONV_DIM,
            D_MODEL,
            D_MODEL)
IN_TOTAL = sum(IN_SIZES)

kernel_name = "hgrn2_shortconv_gated_hybrid"


def _split_points(sizes):
    pts, acc = [], 0
    for s in sizes[:-1]:
        acc += s
        pts.append(acc)
    return pts


def rmsnorm(x, w):
    xf = x.astype(jnp.float32)
    y = xf * lax.rsqrt(jnp.mean(xf * xf, axis=-1, keepdims=True) + EPS)
    return (y * w.astype(jnp.float32)).astype(x.dtype)


def causal_dwconv(x, w, b=None):
    K = w.shape[0]
    L = x.shape[1]
    xp = jnp.pad(x, ((0, 0), (K - 1, 0), (0, 0)))
    y = w[0] * xp[:, 0:L]
    for j in range(1, K):
        y = y + w[j] * xp[:, j:j + L]
    if b is not None:
        y = y + b
    return y


def layer_lower_bounds(lb_param):
    p = jax.nn.softmax(lb_param.astype(jnp.float32), axis=0)
    return jnp.cumsum(p, axis=0)[:DEPTH]


def hgrn2_chunked(q, k, v, logf):
    Bsz, H, T, DK = q.shape
    DV = v.shape[-1]
    n = T // CHUNK

    def to_chunks(a):
        return jnp.moveaxis(a.reshape(Bsz, H, n, CHUNK, a.shape[-1]), 2, 0)

    qc, kc, vc, gc = to_chunks(q), to_chunks(k), to_chunks(v), to_chunks(logf)
    causal = jnp.tril(jnp.ones((CHUNK, CHUNK), dtype=bool))[:, :, None]

    def step(S, inp):
        qb, kb, vb, gb = inp
        G = jnp.cumsum(gb, axis=-2)
        o_inter = jnp.einsum('bhtd,bhdv->bhtv', qb * jnp.exp(G), S)
        diff = G[:, :, :, None, :] - G[:, :, None, :, :]
        decay = jnp.exp(jnp.where(causal, diff, -jnp.inf))
        A = jnp.sum(qb[:, :, :, None, :] * kb[:, :, None, :, :] * decay, axis=-1)
        o = o_inter + jnp.einsum('bhts,bhsv->bhtv', A, vb)
        G_last = G[:, :, -1:, :]
        k_dec = kb * jnp.exp(G_last - G)
        S_new = jnp.exp(G_last[:, :, 0, :])[..., None] * S + jnp.einsum('bhsd,bhsv->bhdv', k_dec, vb)
        return S_new, o

    S0 = jnp.zeros((Bsz, H, DK, DV), jnp.float32)
    _, o = lax.scan(step, S0, (qc, kc, vc, gc))
    return jnp.moveaxis(o, 0, 2).reshape(Bsz, H, T, DV)


def hybrid_layer(h, lb, attn_norm_w, w_in, hgrn_norm_w, conv_w, w_out,
                 ffn_norm_w, w_up, ffn_conv_w, ffn_conv_b, w_down):
    Bsz, L, _ = h.shape
    dt = h.dtype
    u = rmsnorm(h, attn_norm_w)
    proj = u @ w_in
    q, f_raw, i_in, g_out, b_gate, c_gate, x_conv, gate_a, gate_b = jnp.split(
        proj, _split_points(IN_SIZES), axis=-1)

    q = jax.nn.silu(q.astype(jnp.float32))
    f = lb + (1.0 - lb) * jax.nn.sigmoid(f_raw.astype(jnp.float32))
    logf = jnp.log(f)
    k = 1.0 - f
    v = i_in.astype(jnp.float32)

    def heads(a):
        return jnp.transpose(a.reshape(Bsz, L, HGRN_HEADS, -1), (0, 2, 1, 3))

    pad = (-N_META) % CHUNK
    tpad = ((0, 0), (0, 0), (pad, 0), (0, 0))
    qh, kh, vh, gh = (jnp.pad(heads(a), tpad) for a in (q, k, v, logf))
    o = hgrn2_chunked(qh, kh, vh, gh)[:, :, pad:]
    o = jnp.transpose(o, (0, 2, 1, 3))
    o = rmsnorm(o, hgrn_norm_w).reshape(Bsz, L, HGRN_VDIM)
    y_a = (o * jax.nn.silu(g_out.astype(jnp.float32))).astype(dt)

    y_b = b_gate * causal_dwconv(c_gate * x_conv, conv_w)

    merged = jax.nn.sigmoid(gate_a) * y_a + jax.nn.sigmoid(gate_b) * y_b
    h = h + merged @ w_out

    u2 = rmsnorm(h, ffn_norm_w)
    a, val = jnp.split(u2 @ w_up, 2, axis=-1)
    a = causal_dwconv(a, ffn_conv_w, ffn_conv_b)
    h = h + (jax.nn.silu(a) * val) @ w_down
    return h


def setup_inputs(seed: int = 0) -> dict:
    key = jax.random.key(seed)
    ks = jax.random.split(key, 16)
    f32 = jnp.float32
    nrm = lambda k, shape, s: jax.random.normal(k, shape, f32) * s
    return {
        "x": nrm(ks[0], (BATCH, SEQ, D_MODEL), 1.0),
        "meta_tokens": nrm(ks[1], (N_META, D_MODEL), 1.0),
        "lb_param": nrm(ks[2], (DEPTH + 1, HGRN_FDIM), 0.1),
        "attn_norm_w": 1.0 + nrm(ks[3], (DEPTH, D_MODEL), 0.02),
        "w_in": nrm(ks[4], (DEPTH, D_MODEL, IN_TOTAL), D_MODEL ** -0.5),
        "hgrn_norm_w": 1.0 + nrm(ks[5], (DEPTH, HGRN_HEAD_V), 0.02),
        "conv_w": nrm(ks[6], (DEPTH, CONV_WIDTH, CONV_DIM), CONV_WIDTH ** -0.5),
        "w_out": nrm(ks[7], (DEPTH, D_MODEL, D_MODEL), D_MODEL ** -0.5),
        "ffn_norm_w": 1.0 + nrm(ks[8], (DEPTH, D_MODEL), 0.02),
        "w_up": nrm(ks[9], (DEPTH, D_MODEL, 2 * D_FF), D_MODEL ** -0.5),
        "ffn_conv_w": nrm(ks[10], (DEPTH, FFN_CONV_WIDTH, D_FF), FFN_CONV_WIDTH ** -0.5),
        "ffn_conv_b": nrm(ks[11], (DEPTH, D_FF), 0.02),
        "w_down": nrm(ks[12], (DEPTH, D_FF, D_MODEL), D_FF ** -0.5),
        "final_norm_w": 1.0 + nrm(ks[13], (D_MODEL,), 0.02),
    }


def reference(x, meta_tokens, lb_param, attn_norm_w, w_in, hgrn_norm_w, conv_w, w_out,
              ffn_norm_w, w_up, ffn_conv_w, ffn_conv_b, w_down, final_norm_w):
    Bsz = x.shape[0]
    meta = jnp.broadcast_to(meta_tokens.astype(x.dtype)[None], (Bsz, N_META, D_MODEL))
    h = jnp.concatenate([meta, x], axis=1)
    lbs = layer_lower_bounds(lb_param)
    for l in range(DEPTH):
        h = hybrid_layer(h, lbs[l], attn_norm_w[l], w_in[l], hgrn_norm_w[l], conv_w[l],
                         w_out[l], ffn_norm_w[l], w_up[l], ffn_conv_w[l], ffn_conv_b[l],
                         w_down[l])
    h = rmsnorm(h, final_norm_w)
    return h[:, N_META:]
```

```python
import itertools
from contextlib import ExitStack

import numpy as np
import concourse.bass as bass
import concourse.mybir as mybir
from concourse.bass_utils import run_bass_kernel_spmd

F32 = mybir.dt.float32
BF16 = mybir.dt.bfloat16
AF = mybir.ActivationFunctionType
ALU = mybir.AluOpType

SAME_ENGINE_SYNC = True
ENGS = ("pe", "act", "dve", "pool", "sp")


class V:
    __slots__ = ("ap", "key", "ivs")

    def __init__(self, ap, key, ivs):
        self.ap, self.key, self.ivs = ap, key, ivs


class Buf:
    def __init__(self, handle, name, shape):
        self.h, self.name, self.shape = handle, name, list(shape)
        self.fshape = self.shape[1:]
        st, acc = [], 1
        for d in reversed(self.fshape):
            st.append(acc)
            acc *= d
        self.fstr = list(reversed(st))

    def __getitem__(self, idx):
        if not isinstance(idx, tuple):
            idx = (idx,)
        idx = list(idx) + [slice(None)] * (len(self.shape) - len(idx))
        ap = self.h[tuple(idx)]
        rngs = []
        for d, ix in zip(self.fshape, idx[1:]):
            if isinstance(ix, int):
                rngs.append((ix, ix + 1))
            else:
                s = 0 if ix.start is None else ix.start
                e = d if ix.stop is None else ix.stop
                rngs.append((s, e))
        n = len(rngs)
        last = -1
        for i in range(n):
            if rngs[i] != (0, self.fshape[i]):
                last = i
        if last <= 0:
            ivs = [(rngs[0][0] * self.fstr[0], rngs[0][1] * self.fstr[0])]
        else:
            ivs = []
            lead = [range(a, b) for a, b in rngs[:last]]
            for combo in itertools.product(*lead):
                base = sum(c * s for c, s in zip(combo, self.fstr[:last]))
                ivs.append((base + rngs[last][0] * self.fstr[last], base + rngs[last][1] * self.fstr[last]))
        return V(ap, self.name, ivs)


class Op:
    __slots__ = ("eng", "fn", "deps", "pos", "signal", "sigval", "dma", "sem", "semval", "waits", "phase")

    def __init__(self, eng, fn, dma, phase):
        self.eng, self.fn, self.dma, self.phase = eng, fn, dma, phase
        self.deps = set()
        self.signal = False
        self.sigval = None
        self.sem = None
        self.semval = None
        self.waits = []


class Prog:
    def __init__(self, nc, sems_eng, sems_dma):
        self.nc = nc
        self.sems_eng, self.sems_dma = sems_eng, sems_dma
        self.q = {e: [] for e in ENGS}
        self.npos = {e: 0 for e in ENGS}
        self.sigcount = {e: 0 for e in ENGS}
        self.recs = {}
        self.seen = {e: {} for e in ENGS}
        self.dma_streams = {}
        self.phase = 0
        self.phase_dmas = {}

    def _access(self, op, v, is_write):
        recs = self.recs.setdefault(v.key, [])
        for lo, hi in v.ivs:
            new = []
            for r in recs:
                rlo, rhi, rop, rw = r
                if rhi <= lo or rlo >= hi:
                    new.append(r)
                    continue
                if (is_write or rw) and rop is not op:
                    op.deps.add(rop)
                if is_write:
                    if rlo < lo:
                        new.append((rlo, lo, rop, rw))
                    if rhi > hi:
                        new.append((hi, rhi, rop, rw))
                else:
                    if (not rw) and rop.eng == op.eng and (not rop.dma) and (not op.dma) and rlo == lo and rhi == hi:
                        continue
                    new.append(r)
            new.append((lo, hi, op, is_write))
            recs = new
        self.recs[v.key] = recs

    def add(self, eng, fn, reads=(), writes=(), dma=None, extra=()):
        op = Op(eng, fn, dma is not None, self.phase)
        for v in writes:
            self._access(op, v, True)
        for v in reads:
            self._access(op, v, False)
        for d in extra:
            op.deps.add(d)
        op.pos = self.npos[eng]
        self.npos[eng] += 1
        self.q[eng].append(op)
        need = {}
        for d in op.deps:
            if d.phase != self.phase:
                continue
            if d.dma:
                k = ("dma", d.sem)
                if need.get(k, (-1, None))[0] < d.semval:
                    need[k] = (d.semval, d)
            else:
                if d.eng == eng and not op.dma:
                    if eng == "pe" or not SAME_ENGINE_SYNC:
                        continue
                k = ("eng", d.eng)
                if need.get(k, (-1, None))[0] < d.pos:
                    need[k] = (d.pos, d)
        seen = self.seen[eng]
        for k, (val, d) in need.items():
            if seen.get(k, -1) >= val:
                continue
            seen[k] = val
            if not d.dma:
                d.signal = True
            op.waits.append(d)
        if dma is not None:
            cnt = self.dma_streams.get(dma, 0) + 1
            self.dma_streams[dma] = cnt
            op.sem, op.semval = dma, cnt
            self.phase_dmas[dma] = op
        return op

    def end_phase(self):
        if self.phase_dmas:
            self.add("sp", lambda e: e.nop(), extra=list(self.phase_dmas.values()))
        for e in ENGS:
            for op in self.q[e]:
                if (not op.dma) and op.signal:
                    self.sigcount[e] += 1
                    op.sigval = self.sigcount[e]
        se, sd = self.sems_eng, self.sems_dma

        def run(name, eng):
            for op in self.q[name]:
                for d in op.waits:
                    if d.dma:
                        eng.wait_ge(sd[d.sem], 16 * d.semval)
                    else:
                        eng.wait_ge(se[d.eng], d.sigval)
                ins = op.fn(eng)
                if op.dma:
                    ins.then_inc(sd[op.sem], 16)
                elif op.signal:
                    ins.then_inc(se[op.eng], 1)

        with self.nc.Block() as block:
            if self.q["pe"]:
                block.tensor(lambda e: run("pe", e))
            if self.q["act"]:
                block.scalar(lambda e: run("act", e))
            if self.q["dve"]:
                block.vector(lambda e: run("dve", e))
            if self.q["pool"]:
                block.gpsimd(lambda e: run("pool", e))
            if self.q["sp"]:
                block.sync(lambda e: run("sp", e))
        self.q = {e: [] for e in ENGS}
        self.phase += 1
        self.phase_dmas = {}
        self.seen = {e: {} for e in ENGS}


D = 1024
NH = 8
DFF = 2816
NFB = DFF // 128
NMETA = 16
HALO = 16
TLOC = 2048 + HALO
EPS = 1e-6
NCORES = 8
TILES = [(0, 512), (512, 512), (1024, 512), (1536, 512), (2048, 16)]
NTT = 17
FFG = 2
NSTREAM = (["x0", "x1", "c0", "c1", "c2", "wa0", "wa1", "wb0", "wb1", "wo", "wu0", "wu1", "wu2",
            "wd0", "wd1", "cin", "xg", "o0", "o1"] + [f"xp{i}" for i in range(17)])
FFGROUPS = [6, 6, 5, 5]


def tt_rows(i):
    return 128 if i < 16 else 16


def build_program():
    nc = bass.Bass("TRN2", target_bir_lowering=False)

    def din(name, shape):
        return nc.dram_tensor(name, shape, F32, kind="ExternalInput").ap()

    xin = din("xin", [TLOC, D])
    winA1 = din("winA1", [NH, 128, 8, 3, 128])
    winA2 = din("winA2", [NH, 128, 8, 6, 128])
    woutd = din("wout", [128, 8, D])
    wupd = din("wup", [NFB, 128, 8, 2, 128])
    wdnd = din("wdn", [128, NFB, D])
    nrmw = din("nrmw", [128, 3, D])
    cvec = din("cvec", [128, 256])
    cmat = din("cmat", [128, 3, 128])
    outd = nc.dram_tensor("out", [2048, D], F32, kind="ExternalOutput").ap()
    cin = nc.dram_tensor("cin", [128, NH * 129], F32)
    cout = nc.dram_tensor("cout", [NCORES * 128, NH * 129], F32)
    cinv = V(None, "cin", [(0, 1)])
    coutv = V(None, "cout", [(0, 1)])

    with ExitStack() as g:
        sems_eng = {e: g.enter_context(nc.semaphore(f"s_{e}")) for e in ENGS}
        sems_dma = {s: g.enter_context(nc.semaphore(f"d_{s}")) for s in NSTREAM}
        P = Prog(nc, sems_eng, sems_dma)

        def sb(es, name, shape, dt=F32):
            return Buf(es.enter_context(nc.sbuf_tensor(name, shape, dt)), name, shape)

        def ps(es, name, shape, dt=F32):
            return Buf(es.enter_context(nc.psum_tensor(name, shape, dt)), name, shape)

        big = sb(g, "big", [128, NTT * D])
        uT = sb(g, "uT", [128, 8, TLOC], BF16)
        cv = sb(g, "cv", [128, 256])
        maskb = sb(g, "maskb", [128, 4, 128])
        identb = sb(g, "identb", [128, 128], BF16)
        onesb = sb(g, "onesb", [128, 128], BF16)
        lbv = sb(g, "lbv", [128, 4, 8])
        rstd = sb(g, "rstd", [128, NTT])
        L = {}

        C_LP0, C_LP1, C_HNW, C_CONVW, C_FCW, C_FCB, C_KEEP, C_EPS = 0, 8, 16, 17, 41, 107, 129, 140

        def hrow(i, lo=0, hi=D):
            n = tt_rows(i)
            return big[:n, i * D + lo:i * D + hi]

        def oloc(h, t0, n):
            return big[:, h * TLOC + t0:h * TLOC + t0 + n]

        def norm_bufs(es, tag):
            L["nw"] = sb(es, f"nw{tag}", [128, 3, D])
            L["junk"] = sb(es, f"junk{tag}", [128, D], BF16)
            L["ubuf"] = [sb(es, f"ubuf{tag}{i}", [128, D], BF16) for i in range(2)]
            nw_ = L["nw"]
            P.add("sp", lambda e: e.dma_start(out=nw_[:].ap, in_=nrmw), writes=[nw_[:]], dma="c0")

        def rms_stats(which):
            junk = L["junk"]
            P.add("dve", lambda e: e.memset(rstd[:].ap, 0.0), writes=[rstd[:]])
            for i in range(NTT):
                n = tt_rows(i)
                P.add("act", lambda e, i=i, n=n: e.activation(junk[:n].ap, hrow(i).ap, AF.Square, accum_out=rstd[:n, i:i + 1].ap),
                      reads=[hrow(i)], writes=[junk[:], rstd[:, i:i + 1]])
            P.add("dve", lambda e: e.tensor_scalar(rstd[:].ap, rstd[:].ap, 1.0 / D, EPS, ALU.mult, ALU.add), reads=[rstd[:]], writes=[rstd[:]])
            P.add("act", lambda e: e.activation(rstd[:].ap, rstd[:].ap, AF.Sqrt), reads=[rstd[:]], writes=[rstd[:]])
            P.add("dve", lambda e: e.reciprocal(rstd[:].ap, rstd[:].ap), reads=[rstd[:]], writes=[rstd[:]])

        def norm_transpose(which, psT):
            nw, ubuf = L["nw"], L["ubuf"]
            for i in range(NTT):
                n = tt_rows(i)
                ub = ubuf[i % 2]
                pt = psT[i % 2]
                P.add("dve", lambda e, i=i, n=n, ub=ub: e.scalar_tensor_tensor(ub[:n].ap, hrow(i).ap, rstd[:n, i:i + 1].ap, nw[:n, which].ap, ALU.mult, ALU.mult),
                      reads=[hrow(i), rstd[:, i:i + 1], nw[:, which]], writes=[ub[:]])
                for kc in range(8):
                    P.add("pe", lambda e, kc=kc, n=n, ub=ub, pt=pt: e.transpose(pt[:, kc, :n].ap, ub[:n, kc * 128:(kc + 1) * 128].ap, identb[:n, :n].ap),
                          reads=[ub[:], identb[:]], writes=[pt[:, kc]])
                P.add("act", lambda e, i=i, n=n, pt=pt: e.copy(uT[:, :, i * 128:i * 128 + n].ap, pt[:, :, :n].ap),
                      reads=[pt[:]], writes=[uT[:, :, i * 128:i * 128 + n]])

        with ExitStack() as es:
            psT = [ps(es, f"p0T{i}", [128, 8, 128], BF16) for i in range(2)]
            cm = sb(es, "cm", [128, 3, 128])
            norm_bufs(es, "p0")
            P.add("sp", lambda e: e.dma_start(out=cv[:].ap, in_=cvec), writes=[cv[:]], dma="c1")
            P.add("sp", lambda e: e.dma_start(out=cm[:].ap, in_=cmat), writes=[cm[:]], dma="c2")
            for i in range(NTT):
                n = tt_rows(i)
                P.add("sp", lambda e, i=i, n=n: e.dma_start(out=hrow(i).ap, in_=xin[i * 128:i * 128 + n, :]),
                      writes=[hrow(i)], dma=f"xp{i}")
            for c in range(4):
                P.add("dve", lambda e, c=c: e.tensor_copy(maskb[:, c].ap, cm[:, 0].ap), reads=[cm[:, 0]], writes=[maskb[:, c]])
            P.add("dve", lambda e: e.tensor_copy(identb[:].ap, cm[:, 1].ap), reads=[cm[:, 1]], writes=[identb[:]])
            P.add("dve", lambda e: e.tensor_copy(onesb[:].ap, cm[:, 2].ap), reads=[cm[:, 2]], writes=[onesb[:]])
            P.add("dve", lambda e: e.tensor_tensor(lbv[:, 3].ap, cv[:, C_LP1:C_LP1 + 8].ap, cv[:, C_LP0:C_LP0 + 8].ap, ALU.subtract),
                  reads=[cv[:]], writes=[lbv[:, 3]])
            P.add("act", lambda e: e.activation(lbv[:, 3].ap, lbv[:, 3].ap, AF.Exp), reads=[lbv[:, 3]], writes=[lbv[:, 3]])
            P.add("dve", lambda e: e.tensor_scalar(lbv[:, 0].ap, lbv[:, 3].ap, 1.0, None, ALU.add), reads=[lbv[:, 3]], writes=[lbv[:, 0]])
            P.add("dve", lambda e: e.reciprocal(lbv[:, 0].ap, lbv[:, 0].ap), reads=[lbv[:, 0]], writes=[lbv[:, 0]])
            P.add("dve", lambda e: e.tensor_tensor(lbv[:, 1].ap, lbv[:, 3].ap, lbv[:, 0].ap, ALU.mult),
                  reads=[lbv[:, 3], lbv[:, 0]], writes=[lbv[:, 1]])
            P.add("dve", lambda e: e.tensor_scalar(lbv[:, 2].ap, lbv[:, 1].ap, -1.0, None, ALU.mult), reads=[lbv[:, 1]], writes=[lbv[:, 2]])
            rms_stats(0)
            norm_transpose(0, psT)
            P.end_phase()

        with ExitStack() as gm:
            qgm = sb(gm, "qgm", [128, 8, TLOC], BF16)
            sinb = sb(gm, "sinb", [128, NH, 128], BF16)
            xch = sb(gm, "xch", [128, NH, 129])

            with ExitStack() as es:
                wA = [sb(es, f"wA{i}", [128, 8, 3, 128], BF16) for i in range(2)]
                sf = sb(es, "sf", [128, 512]); sq = sb(es, "sq", [128, 512]); logf = sb(es, "logf", [128, 512])
                kk = sb(es, "kk", [128, 512]); grel = sb(es, "grel", [128, 512]); eng_ = sb(es, "eng", [128, 512])
                siluq = sb(es, "siluq", [128, 512])
                egall = sb(es, "egall", [128, TLOC])
                ones32 = sb(es, "ones32", [128, 128])
                kgT = [sb(es, f"kgT{i}", [128, 512], BF16) for i in range(2)]
                qgT = [sb(es, f"qgT{i}", [128, 512], BF16) for i in range(2)]
                vtok = [sb(es, f"vtok{i}", [128, 4, 128], BF16) for i in range(2)]
                kgtok = [sb(es, f"kgtok{i}", [128, 4, 128], BF16) for i in range(2)]
                AT = [sb(es, f"AT{i}", [128, 4, 128], BF16) for i in range(2)]
                Tst = sb(es, "Tst", [128, 128])
                Sst = sb(es, "Sst", [128, 18, 128], BF16)
                dcum = sb(es, "dcum", [128, 2])
                psq = ps(es, "psq", [128, 512]); psf = ps(es, "psf", [128, 512]); psv = ps(es, "psv", [128, 4, 128])
                pskt = ps(es, "pskt", [128, 4, 128], BF16); psA = ps(es, "psA", [128, 4, 128])
                pso = [ps(es, f"pso{i}", [128, 512]) for i in range(2)]
                psS4 = ps(es, "psS4", [128, 4, 128])
                P.add("dve", lambda e: e.memset(ones32[:].ap, 1.0), writes=[ones32[:]])

                def load_wA(h):
                    w = wA[h % 2]
                    P.add("pool", lambda e, h=h, w=w: e.dma_start(out=w[:].ap, in_=winA1[h]), writes=[w[:]], dma=f"wa{h % 2}")

                def geo(ti):
                    t0, n = TILES[ti]
                    return t0, n, max(1, n // 128), min(128, n)

                def projQF(h, ti, grp):
                    t0, n, nch, ncs = geo(ti)
                    w = wA[h % 2]
                    pp = psq if grp == 0 else psf
                    for kc in range(8):
                        P.add("pe", lambda e, kc=kc: e.matmul(pp[:, :n].ap, w[:, kc, grp, :].ap, uT[:, kc, t0:t0 + n].ap, start=(kc == 0), stop=(kc == 7)),
                              reads=[w[:, kc, grp, :], uT[:, kc, t0:t0 + n]], writes=[pp[:, :n]])

                def projV(h, ti):
                    t0, n, nch, ncs = geo(ti)
                    w = wA[h % 2]
                    for c in range(nch):
                        for kc in range(8):
                            P.add("pe", lambda e, kc=kc, c=c: e.matmul(
                                psv[:ncs, c, :].ap, uT[:, kc, t0 + c * 128:t0 + c * 128 + ncs].ap, w[:, kc, 2, :].ap, start=(kc == 0), stop=(kc == 7)),
                                reads=[w[:, kc, 2, :], uT[:, kc, t0 + c * 128:t0 + c * 128 + ncs]], writes=[psv[:, c]])

                def elem1a(h, ti, tp):
                    t0, n, nch, ncs = geo(ti)
                    lb_ap, oml_ap, noml_ap = lbv[:, 0, h:h + 1], lbv[:, 1, h:h + 1], lbv[:, 2, h:h + 1]
                    vt = vtok[tp]
                    P.add("act", lambda e: e.activation(sf[:, :n].ap, psf[:, :n].ap, AF.Sigmoid), reads=[psf[:, :n]], writes=[sf[:, :n]])
                    P.add("act", lambda e: e.activation(sq[:, :n].ap, psq[:, :n].ap, AF.Sigmoid), reads=[psq[:, :n]], writes=[sq[:, :n]])
                    P.add("act", lambda e: e.activation(logf[:, :n].ap, sf[:, :n].ap, AF.Ln, bias=lb_ap.ap, scale=oml_ap.ap),
                          reads=[sf[:, :n], lbv[:]], writes=[logf[:, :n]])
                    P.add("dve", lambda e: e.tensor_copy(vt[:ncs, :nch].ap, psv[:ncs, :nch].ap), reads=[psv[:, :nch]], writes=[vt[:, :nch]])
                    P.add("dve", lambda e: e.tensor_tensor(siluq[:, :n].ap, psq[:, :n].ap, sq[:, :n].ap, ALU.mult),
                          reads=[psq[:, :n], sq[:, :n]], writes=[siluq[:, :n]])
                    for c in range(nch):
                        a, b = c * 128, c * 128 + ncs
                        P.add("dve", lambda e, a=a, b=b: e.tensor_tensor_scan(grel[:, a:b].ap, ones32[:, :b - a].ap, logf[:, a:b].ap, 0.0, ALU.mult, ALU.add),
                              reads=[ones32[:], logf[:, a:b]], writes=[grel[:, a:b]])
                    P.add("dve", lambda e: e.tensor_scalar(kk[:, :n].ap, sf[:, :n].ap, noml_ap.ap, oml_ap.ap, ALU.mult, ALU.add),
                          reads=[sf[:, :n], lbv[:]], writes=[kk[:, :n]])

                def elem1b(h, ti, tp):
                    t0, n, nch, ncs = geo(ti)
                    kg, qg = kgT[tp], qgT[tp]
                    P.add("act", lambda e: e.activation(eng_[:, :n].ap, grel[:, :n].ap, AF.Exp, scale=-1.0), reads=[grel[:, :n]], writes=[eng_[:, :n]])
                    P.add("act", lambda e: e.activation(egall[:, t0:t0 + n].ap, grel[:, :n].ap, AF.Exp), reads=[grel[:, :n]], writes=[egall[:, t0:t0 + n]])
                    P.add("dve", lambda e: e.tensor_tensor(kg[:, :n].ap, kk[:, :n].ap, eng_[:, :n].ap, ALU.mult),
                          reads=[kk[:, :n], eng_[:, :n]], writes=[kg[:, :n]])
                    P.add("dve", lambda e: e.tensor_tensor(qg[:, :n].ap, siluq[:, :n].ap, egall[:, t0:t0 + n].ap, ALU.mult),
                          reads=[siluq[:, :n], egall[:, t0:t0 + n]], writes=[qg[:, :n]])

                def s3a(h, ti, tp):
                    t0, n, nch, ncs = geo(ti)
                    kg, qg = kgT[tp], qgT[tp]
                    for c in range(nch):
                        a = c * 128
                        P.add("pe", lambda e, c=c, a=a: e.transpose(pskt[:ncs, c, :].ap, kg[:, a:a + ncs].ap, identb[:].ap),
                              reads=[kg[:, a:a + ncs], identb[:]], writes=[pskt[:, c]])
                    for c in range(nch):
                        a = c * 128
                        P.add("pe", lambda e, c=c, a=a: e.matmul(psA[:ncs, c, :ncs].ap, kg[:, a:a + ncs].ap, qg[:, a:a + ncs].ap, start=True, stop=True),
                              reads=[kg[:, a:a + ncs], qg[:, a:a + ncs]], writes=[psA[:, c]])

                def s3b(h, ti, tp):
                    t0, n, nch, ncs = geo(ti)
                    kt, at = kgtok[tp], AT[tp]
                    P.add("act", lambda e: e.copy(kt[:ncs, :nch].ap, pskt[:ncs, :nch].ap), reads=[pskt[:, :nch]], writes=[kt[:, :nch]])
                    P.add("dve", lambda e: e.tensor_tensor(at[:ncs, :nch, :ncs].ap, psA[:ncs, :nch, :ncs].ap, maskb[:ncs, :nch, :ncs].ap, ALU.mult),
                          reads=[psA[:, :nch], maskb[:]], writes=[at[:, :nch]])

                def s3c(h, ti, tp):
                    t0, n, nch, ncs = geo(ti)
                    kt, vt = kgtok[tp], vtok[tp]
                    for c in range(nch):
                        P.add("pe", lambda e, c=c: e.matmul(psS4[:, c, :].ap, kt[:ncs, c, :].ap, vt[:ncs, c, :].ap, start=True, stop=True),
                              reads=[kt[:, c], vt[:, c]], writes=[psS4[:, c]])

                def s4(h, ti, tp):
                    t0, n, nch, ncs = geo(ti)
                    for c in range(nch):
                        a = c * 128
                        ci = ti * 4 + c
                        dec = egall[:, t0 + a + ncs - 1:t0 + a + ncs]
                        if ci == 0:
                            P.add("dve", lambda e, c=c: e.tensor_copy(Tst[:].ap, psS4[:, c].ap), reads=[psS4[:]], writes=[Tst[:]])
                        else:
                            pa = t0 + a - 1
                            decp = egall[:, pa:pa + 1]
                            P.add("dve", lambda e, c=c, decp=decp: e.scalar_tensor_tensor(Tst[:].ap, Tst[:].ap, decp.ap, psS4[:, c].ap, ALU.mult, ALU.add),
                                  reads=[Tst[:], decp, psS4[:]], writes=[Tst[:]])
                        P.add("dve", lambda e, ci=ci, dec=dec: e.tensor_scalar(Sst[:, ci + 1].ap, Tst[:].ap, dec.ap, None, ALU.mult),
                              reads=[Tst[:], dec], writes=[Sst[:, ci + 1]])
                        if ci == 15:
                            P.add("dve", lambda e, dec=dec: e.tensor_scalar(xch[:, h, 0:128].ap, Tst[:].ap, dec.ap, None, ALU.mult),
                                  reads=[Tst[:], dec], writes=[xch[:, h, 0:128]])

                def s5(h, ti, tp):
                    t0, n, nch, ncs = geo(ti)
                    qg, vt, at, po = qgT[tp], vtok[tp], AT[tp], pso[tp]
                    for c in range(nch):
                        a = c * 128
                        ci = ti * 4 + c
                        if ci > 0:
                            P.add("pe", lambda e, a=a, ci=ci: e.matmul(po[:, a:a + ncs].ap, Sst[:, ci].ap, qg[:, a:a + ncs].ap, start=True, stop=False),
                                  reads=[Sst[:, ci], qg[:, a:a + ncs]], writes=[po[:, a:a + ncs]])
                        P.add("pe", lambda e, c=c, a=a, ci=ci: e.matmul(po[:, a:a + ncs].ap, vt[:ncs, c, :].ap, at[:ncs, c, :ncs].ap, start=(ci == 0), stop=True),
                              reads=[vt[:, c], at[:, c]], writes=[po[:, a:a + ncs]])

                def s6(h, ti, tp):
                    t0, n, nch, ncs = geo(ti)
                    qg, po = qgT[tp], pso[tp]
                    P.add("act", lambda e: e.copy(oloc(h, t0, n).ap, po[:, :n].ap), reads=[po[:, :n]], writes=[oloc(h, t0, n)])
                    for c in range(nch):
                        a = c * 128
                        ci = ti * 4 + c
                        dec = egall[:, t0 + a + ncs - 1:t0 + a + ncs]
                        if ci == 0:
                            P.add("dve", lambda e, a=a: e.tensor_copy(qgm[:, h, t0 + a:t0 + a + ncs].ap, qg[:, a:a + ncs].ap),
                                  reads=[qg[:, a:a + ncs]], writes=[qgm[:, h, t0 + a:t0 + a + ncs]])
                            P.add("dve", lambda e, dec=dec: e.tensor_copy(dcum[:, 0:1].ap, dec.ap), reads=[dec], writes=[dcum[:]])
                        else:
                            P.add("dve", lambda e, a=a: e.tensor_scalar(qgm[:, h, t0 + a:t0 + a + ncs].ap, qg[:, a:a + ncs].ap, dcum[:, 0:1].ap, None, ALU.mult),
                                  reads=[qg[:, a:a + ncs], dcum[:]], writes=[qgm[:, h, t0 + a:t0 + a + ncs]])
                            P.add("dve", lambda e, dec=dec: e.tensor_tensor(dcum[:, 0:1].ap, dcum[:, 0:1].ap, dec.ap, ALU.mult), reads=[dec, dcum[:]], writes=[dcum[:]])
                        if ci == 15:
                            P.add("dve", lambda e: e.tensor_copy(xch[:, h, 128:129].ap, dcum[:, 0:1].ap), reads=[dcum[:]], writes=[xch[:, h, 128:129]])

                load_wA(0)
                load_wA(1)
                seq = [(h, ti) for h in range(NH) for ti in range(len(TILES))]
                projQF(*seq[0], 0); projQF(*seq[0], 1); projV(*seq[0])
                pend = None
                for idx, (h, ti) in enumerate(seq):
                    tp = idx % 2
                    nxt = seq[idx + 1] if idx + 1 < len(seq) else None
                    elem1a(h, ti, tp)
                    if pend is not None:
                        s6(*pend)
                    elem1b(h, ti, tp)
                    if nxt:
                        projQF(*nxt, 0)
                    s3a(h, ti, tp)
                    s3b(h, ti, tp)
                    if nxt:
                        projQF(*nxt, 1)
                    s3c(h, ti, tp)
                    s4(h, ti, tp)
                    if nxt:
                        projV(*nxt)
                    s5(h, ti, tp)
                    pend = (h, ti, tp)
                    if nxt and nxt[0] != h and h + 2 < NH:
                        load_wA(h + 2)
                s6(*pend)
                P.end_phase()

            with ExitStack() as es:
                xg = sb(es, "xg", [128, NCORES, NH * 129])
                acc = sb(es, "acc", [128, NH, 128])
                deff = sb(es, "deff", [128, NCORES, NH])
                P.add("sp", lambda e: e.dma_start(out=cin.ap().rearrange("p (h n) -> p h n", h=NH), in_=xch[:].ap),
                      reads=[xch[:]], writes=[cinv], dma="cin")

                def cc(e):
                    return e.collective_compute("AllGather", ALU.bypass, replica_groups=[list(range(NCORES))],
                                                ins=[cin.ap().opt()], outs=[cout.ap().opt()])
                P.add("pool", cc, reads=[cinv], writes=[coutv])
                P.add("sp", lambda e: e.dma_start(out=xg[:].ap, in_=cout.ap().rearrange("(r p) n -> p r n", p=128)),
                      reads=[coutv], writes=[xg[:]], dma="xg")
                P.add("dve", lambda e: e.memset(acc[:].ap, 0.0), writes=[acc[:]])

                def xg3(r):
                    return xg[:, r].ap.rearrange("p (h n) -> p h n", h=NH)

                for r in range(NCORES):
                    keep = cv[:, C_KEEP + r:C_KEEP + r + 1]
                    dd = deff[:, r]
                    P.add("dve", lambda e, r=r, dd=dd: e.tensor_scalar(dd.ap, xg3(r)[:, :, 128], -1.0, None, ALU.add), reads=[xg[:, r]], writes=[dd])
                    P.add("dve", lambda e, dd=dd, keep=keep: e.tensor_scalar(dd.ap, dd.ap, keep.ap, 1.0, ALU.mult, ALU.add), reads=[dd, keep], writes=[dd])
                    for h in range(NH):
                        ddh = deff[:, r, h:h + 1]
                        P.add("dve", lambda e, h=h, ddh=ddh: e.tensor_scalar(acc[:, h].ap, acc[:, h].ap, ddh.ap, None, ALU.mult),
                              reads=[acc[:, h], ddh], writes=[acc[:, h]])
                    P.add("dve", lambda e, r=r, keep=keep: e.scalar_tensor_tensor(acc[:].ap, xg3(r)[:, :, 0:128], keep.ap, acc[:].ap, ALU.mult, ALU.add),
                          reads=[xg[:, r], keep, acc[:]], writes=[acc[:]])
                P.add("act", lambda e: e.copy(sinb[:].ap, acc[:].ap), reads=[acc[:]], writes=[sinb[:]])
                P.end_phase()

            with ExitStack() as es:
                osq = [sb(es, f"osq{i}", [128, 512], BF16) for i in range(2)]
                lnv = [sb(es, f"lnv{i}", [128, 512]) for i in range(2)]
                rsv = [sb(es, f"rsv{i}", [128, 512]) for i in range(2)]
                psc = [ps(es, f"psc{i}", [128, 512]) for i in range(2)]
                pss = [ps(es, f"pss{i}", [128, 512]) for i in range(2)]
                k = 0
                for h in range(NH):
                    for (t0, n) in TILES:
                        pc, pq, oq, lv, rv = psc[k % 2], pss[k % 2], osq[k % 2], lnv[k % 2], rsv[k % 2]
                        k += 1
                        P.add("pe", lambda e, h=h, t0=t0, n=n, pc=pc: e.matmul(pc[:, :n].ap, sinb[:, h, :].ap, qgm[:, h, t0:t0 + n].ap, start=True, stop=True),
                              reads=[sinb[:, h], qgm[:, h, t0:t0 + n]], writes=[pc[:, :n]])
                        P.add("dve", lambda e, h=h, t0=t0, n=n, pc=pc: e.tensor_tensor(oloc(h, t0, n).ap, oloc(h, t0, n).ap, pc[:, :n].ap, ALU.add),
                              reads=[oloc(h, t0, n), pc[:, :n]], writes=[oloc(h, t0, n)])
                        P.add("act", lambda e, h=h, t0=t0, n=n, oq=oq: e.activation(oq[:, :n].ap, oloc(h, t0, n).ap, AF.Square),
                              reads=[oloc(h, t0, n)], writes=[oq[:, :n]])
                        P.add("pe", lambda e, n=n, pq=pq, oq=oq: e.matmul(pq[:, :n].ap, onesb[:].ap, oq[:, :n].ap, start=True, stop=True),
                              reads=[onesb[:], oq[:, :n]], writes=[pq[:, :n]])
                        P.add("act", lambda e, n=n, pq=pq, lv=lv: e.activation(lv[:, :n].ap, pq[:, :n].ap, AF.Ln, bias=cv[:, C_EPS:C_EPS + 1].ap, scale=1.0 / 128),
                              reads=[pq[:, :n], cv[:]], writes=[lv[:, :n]])
                        P.add("act", lambda e, n=n, lv=lv, rv=rv: e.activation(rv[:, :n].ap, lv[:, :n].ap, AF.Exp, scale=-0.5),
                              reads=[lv[:, :n]], writes=[rv[:, :n]])
                        P.add("dve", lambda e, h=h, t0=t0, n=n, rv=rv: e.tensor_tensor(oloc(h, t0, n).ap, oloc(h, t0, n).ap, rv[:, :n].ap, ALU.mult),
                              reads=[oloc(h, t0, n), rv[:, :n]], writes=[oloc(h, t0, n)])
                P.end_phase()

            with ExitStack() as es:
                wB = [sb(es, f"wB{i}", [128, 8, 6, 128], BF16) for i in range(2)]
                sg = sb(es, "sg", [128, 512]); sga = sb(es, "sga", [128, 512]); sgb = sb(es, "sgb", [128, 512])
                csb = sb(es, "csb", [128, 512]); cx = sb(es, "cx", [128, 514]); cvv = sb(es, "cvv", [128, 512])
                yb = sb(es, "yb", [128, 512]); gate = sb(es, "gate", [128, 512]); ya = sb(es, "ya", [128, 512])
                psG = [ps(es, f"psG{i}", [128, 512]) for i in range(2)]
                psB = [ps(es, f"psB{i}", [128, 512]) for i in range(2)]
                psC = ps(es, "psC", [128, 512]); psX = ps(es, "psX", [128, 512])
                psga = ps(es, "psga", [128, 512]); psgb = ps(es, "psgb", [128, 512])

                def load_wB(cb):
                    w = wB[cb % 2]
                    P.add("pool", lambda e, cb=cb, w=w: e.dma_start(out=w[:].ap, in_=winA2[cb]), writes=[w[:]], dma=f"wb{cb % 2}")

                load_wB(0)
                load_wB(1)
                k = 0
                for cb in range(NH):
                    w = wB[cb % 2]
                    P.add("dve", lambda e: e.memset(cx[:, 0:2].ap, 0.0), writes=[cx[:, 0:2]])
                    for (t0, n) in TILES:
                        pg, pb = psG[k % 2], psB[k % 2]
                        k += 1
                        pmap = {0: pg, 1: pb, 2: psC, 3: psX, 4: psga, 5: psgb}
                        for grp in (2, 3, 0, 4, 1, 5):
                            pp = pmap[grp]
                            for kc in range(8):
                                P.add("pe", lambda e, kc=kc, grp=grp, pp=pp, w=w, t0=t0, n=n: e.matmul(
                                    pp[:, :n].ap, w[:, kc, grp, :].ap, uT[:, kc, t0:t0 + n].ap, start=(kc == 0), stop=(kc == 7)),
                                    reads=[w[:, kc, grp, :], uT[:, kc, t0:t0 + n]], writes=[pp[:, :n]])
                        cw = [cv[:, C_CONVW + cb * 3 + j:C_CONVW + cb * 3 + j + 1] for j in range(3)]
                        hnw = cv[:, C_HNW:C_HNW + 1]
                        P.add("act", lambda e, n=n: e.copy(csb[:, :n].ap, psC[:, :n].ap), reads=[psC[:, :n]], writes=[csb[:, :n]])
                        P.add("dve", lambda e, n=n: e.tensor_tensor(cx[:, 2:2 + n].ap, csb[:, :n].ap, psX[:, :n].ap, ALU.mult),
                              reads=[csb[:, :n], psX[:, :n]], writes=[cx[:, 2:2 + n]])
                        P.add("act", lambda e, n=n, pg=pg: e.activation(sg[:, :n].ap, pg[:, :n].ap, AF.Sigmoid), reads=[pg[:, :n]], writes=[sg[:, :n]])
                        P.add("dve", lambda e, n=n, pg=pg: e.tensor_tensor(gate[:, :n].ap, sg[:, :n].ap, pg[:, :n].ap, ALU.mult),
                              reads=[sg[:, :n], pg[:, :n]], writes=[gate[:, :n]])
                        P.add("act", lambda e, n=n: e.activation(sga[:, :n].ap, psga[:, :n].ap, AF.Sigmoid), reads=[psga[:, :n]], writes=[sga[:, :n]])
                        P.add("act", lambda e, n=n: e.activation(sgb[:, :n].ap, psgb[:, :n].ap, AF.Sigmoid), reads=[psgb[:, :n]], writes=[sgb[:, :n]])
                        P.add("dve", lambda e, n=n, cw=cw: e.tensor_scalar(cvv[:, :n].ap, cx[:, 0:n].ap, cw[0].ap, None, ALU.mult),
                              reads=[cx[:, 0:n], cv[:]], writes=[cvv[:, :n]])
                        P.add("dve", lambda e, n=n, cw=cw: e.scalar_tensor_tensor(cvv[:, :n].ap, cx[:, 1:1 + n].ap, cw[1].ap, cvv[:, :n].ap, ALU.mult, ALU.add),
                              reads=[cx[:, 1:1 + n], cv[:], cvv[:, :n]], writes=[cvv[:, :n]])
                        P.add("dve", lambda e, n=n, cw=cw: e.scalar_tensor_tensor(cvv[:, :n].ap, cx[:, 2:2 + n].ap, cw[2].ap, cvv[:, :n].ap, ALU.mult, ALU.add),
                              reads=[cx[:, 2:2 + n], cv[:], cvv[:, :n]], writes=[cvv[:, :n]])
                        P.add("dve", lambda e, n=n: e.tensor_copy(cx[:, 0:2].ap, cx[:, n:n + 2].ap), reads=[cx[:, n:n + 2]], writes=[cx[:, 0:2]])
                        P.add("dve", lambda e, n=n, pb=pb: e.tensor_tensor(yb[:, :n].ap, cvv[:, :n].ap, pb[:, :n].ap, ALU.mult),
                              reads=[cvv[:, :n], pb[:, :n]], writes=[yb[:, :n]])
                        P.add("dve", lambda e, n=n: e.tensor_tensor(yb[:, :n].ap, yb[:, :n].ap, sgb[:, :n].ap, ALU.mult),
                              reads=[yb[:, :n], sgb[:, :n]], writes=[yb[:, :n]])
                        P.add("dve", lambda e, n=n, hnw=hnw: e.scalar_tensor_tensor(gate[:, :n].ap, gate[:, :n].ap, hnw.ap, sga[:, :n].ap, ALU.mult, ALU.mult),
                              reads=[gate[:, :n], cv[:], sga[:, :n]], writes=[gate[:, :n]])
                        P.add("dve", lambda e, cb=cb, t0=t0, n=n: e.tensor_tensor(ya[:, :n].ap, oloc(cb, t0, n).ap, gate[:, :n].ap, ALU.mult),
                              reads=[oloc(cb, t0, n), gate[:, :n]], writes=[ya[:, :n]])
                        P.add("dve", lambda e, cb=cb, t0=t0, n=n: e.tensor_tensor(qgm[:, cb, t0:t0 + n].ap, ya[:, :n].ap, yb[:, :n].ap, ALU.add),
                              reads=[ya[:, :n], yb[:, :n]], writes=[qgm[:, cb, t0:t0 + n]])
                    if cb + 2 < NH:
                        load_wB(cb + 2)
                P.end_phase()

            with ExitStack() as es:
                wo = sb(es, "wo", [128, 8, D], BF16)
                norm_bufs(es, "p3")
                xt = [sb(es, f"xt{i}", [128, D]) for i in range(2)]
                psh = [ps(es, f"psh{i}", [128, 512]) for i in range(4)]
                psT = [ps(es, f"p3T{i}", [128, 8, 128], BF16) for i in range(2)]
                P.add("pool", lambda e: e.dma_start(out=wo[:].ap, in_=woutd), writes=[wo[:]], dma="wo")
                for i in range(NTT):
                    n = tt_rows(i)
                    x_ = xt[i % 2]
                    P.add("sp", lambda e, i=i, n=n, x_=x_: e.dma_start(out=x_[:n].ap, in_=xin[i * 128:i * 128 + n, :]), writes=[x_[:]], dma=f"x{i % 2}")
                    for half in range(2):
                        pp = psh[(i % 2) * 2 + half]
                        for kc in range(8):
                            P.add("pe", lambda e, i=i, n=n, kc=kc, half=half, pp=pp: e.matmul(
                                pp[:n].ap, qgm[:, kc, i * 128:i * 128 + n].ap, wo[:, kc, half * 512:(half + 1) * 512].ap, start=(kc == 0), stop=(kc == 7)),
                                reads=[qgm[:, kc, i * 128:i * 128 + n], wo[:, kc, half * 512:(half + 1) * 512]], writes=[pp[:]])
                        P.add("dve", lambda e, i=i, n=n, half=half, pp=pp, x_=x_: e.tensor_tensor(
                            hrow(i, half * 512, (half + 1) * 512).ap, pp[:n].ap, x_[:n, half * 512:(half + 1) * 512].ap, ALU.add),
                            reads=[pp[:], x_[:]], writes=[hrow(i, half * 512, (half + 1) * 512)])
                rms_stats(1)
                norm_transpose(1, psT)
                P.end_phase()

        with ExitStack() as es:
            NGM = max(FFGROUPS)
            gT = sb(es, "gT", [128, NGM, TLOC], BF16)
            wu = [sb(es, f"wu{i}", [128, 8, 2, 128], BF16) for i in range(3)]
            wdg = [sb(es, f"wdg{i}", [128, NGM, D], BF16) for i in range(2)]
            araw = sb(es, "araw", [128, 514]); ac = sb(es, "ac", [128, 512]); sa = sb(es, "sa", [128, 512])
            psa = [ps(es, f"psa{i}", [128, 512]) for i in range(2)]
            psv2 = [ps(es, f"psv2{i}", [128, 512]) for i in range(2)]
            psd = [ps(es, f"psd{i}", [128, 512]) for i in range(4)]

            def load_wu(blk):
                w = wu[blk % 3]
                P.add("pool", lambda e, blk=blk, w=w: e.dma_start(out=w[:].ap, in_=wupd[blk]), writes=[w[:]], dma=f"wu{blk % 3}")

            def load_wd(gi, blk0, ng):
                w = wdg[gi % 2]
                P.add("pool", lambda e, blk0=blk0, ng=ng, w=w: e.dma_start(out=w[:, :ng, :].ap, in_=wdnd[:, blk0:blk0 + ng, :]),
                      writes=[w[:, :ng, :]], dma=f"wd{gi % 2}")

            for b_ in range(3):
                load_wu(b_)
            gstart = [sum(FFGROUPS[:i]) for i in range(len(FFGROUPS))]
            load_wd(0, gstart[0], FFGROUPS[0])
            k = 0
            for gi, ng in enumerate(FFGROUPS):
                blk0 = gstart[gi]
                if gi + 1 < len(FFGROUPS):
                    load_wd(gi + 1, gstart[gi + 1], FFGROUPS[gi + 1])
                for bl in range(ng):
                    blk = blk0 + bl
                    w = wu[blk % 3]
                    fw = [cv[:, C_FCW + blk * 3 + j:C_FCW + blk * 3 + j + 1] for j in range(3)]
                    fb = cv[:, C_FCB + blk:C_FCB + blk + 1]
                    P.add("dve", lambda e: e.memset(araw[:, 0:2].ap, 0.0), writes=[araw[:, 0:2]])
                    for (t0, n) in TILES:
                        pa, pv = psa[k % 2], psv2[k % 2]
                        k += 1
                        for gsel, pp in ((0, pa), (1, pv)):
                            for kc in range(8):
                                P.add("pe", lambda e, kc=kc, gsel=gsel, pp=pp, w=w, t0=t0, n=n: e.matmul(
                                    pp[:, :n].ap, w[:, kc, gsel, :].ap, uT[:, kc, t0:t0 + n].ap, start=(kc == 0), stop=(kc == 7)),
                                    reads=[w[:, kc, gsel, :], uT[:, kc, t0:t0 + n]], writes=[pp[:, :n]])
                        P.add("act", lambda e, n=n, pa=pa: e.copy(araw[:, 2:2 + n].ap, pa[:, :n].ap), reads=[pa[:, :n]], writes=[araw[:, 2:2 + n]])
                        P.add("dve", lambda e, n=n, fw=fw, fb=fb: e.tensor_scalar(ac[:, :n].ap, araw[:, 0:n].ap, fw[0].ap, fb.ap, ALU.mult, ALU.add),
                              reads=[araw[:, 0:n], cv[:]], writes=[ac[:, :n]])
                        P.add("dve", lambda e, n=n, fw=fw: e.scalar_tensor_tensor(ac[:, :n].ap, araw[:, 1:1 + n].ap, fw[1].ap, ac[:, :n].ap, ALU.mult, ALU.add),
                              reads=[araw[:, 1:1 + n], cv[:], ac[:, :n]], writes=[ac[:, :n]])
                        P.add("dve", lambda e, n=n, fw=fw: e.scalar_tensor_tensor(ac[:, :n].ap, araw[:, 2:2 + n].ap, fw[2].ap, ac[:, :n].ap, ALU.mult, ALU.add),
                              reads=[araw[:, 2:2 + n], cv[:], ac[:, :n]], writes=[ac[:, :n]])
                        P.add("dve", lambda e, n=n: e.tensor_copy(araw[:, 0:2].ap, araw[:, n:n + 2].ap), reads=[araw[:, n:n + 2]], writes=[araw[:, 0:2]])
                        P.add("act", lambda e, n=n: e.activation(sa[:, :n].ap, ac[:, :n].ap, AF.Silu), reads=[ac[:, :n]], writes=[sa[:, :n]])
                        P.add("dve", lambda e, bl=bl, t0=t0, n=n, pv=pv: e.tensor_tensor(gT[:, bl, t0:t0 + n].ap, sa[:, :n].ap, pv[:, :n].ap, ALU.mult),
                              reads=[sa[:, :n], pv[:, :n]], writes=[gT[:, bl, t0:t0 + n]])
                    if blk + 3 < NFB:
                        load_wu(blk + 3)
                wdw = wdg[gi % 2]
                for i in range(NTT):
                    n = tt_rows(i)
                    for half in range(2):
                        pp = psd[(i % 2) * 2 + half]
                        for bl in range(ng):
                            P.add("pe", lambda e, i=i, n=n, bl=bl, half=half, pp=pp, wdw=wdw, ng=ng: e.matmul(
                                pp[:n].ap, gT[:, bl, i * 128:i * 128 + n].ap, wdw[:, bl, half * 512:(half + 1) * 512].ap, start=(bl == 0), stop=(bl == ng - 1)),
                                reads=[gT[:, bl, i * 128:i * 128 + n], wdw[:, bl, half * 512:(half + 1) * 512]], writes=[pp[:]])
                        P.add("dve", lambda e, i=i, n=n, half=half, pp=pp: e.tensor_tensor(
                            hrow(i, half * 512, (half + 1) * 512).ap, hrow(i, half * 512, (half + 1) * 512).ap, pp[:n].ap, ALU.add),
                            reads=[pp[:], hrow(i, half * 512, (half + 1) * 512)], writes=[hrow(i, half * 512, (half + 1) * 512)])
            P.end_phase()

        with ExitStack() as es:
            ob = [sb(es, f"ob{i}", [128, D]) for i in range(2)]
            norm_bufs(es, "p5")
            nw = L["nw"]
            rms_stats(2)
            outs = []
            for i in range(NTT):
                n = tt_rows(i)
                o_ = ob[i % 2]
                P.add("dve", lambda e, i=i, n=n, o_=o_: e.scalar_tensor_tensor(o_[:n].ap, hrow(i).ap, rstd[:n, i:i + 1].ap, nw[:n, 2].ap, ALU.mult, ALU.mult),
                      reads=[hrow(i), rstd[:, i:i + 1], nw[:, 2]], writes=[o_[:]])
                if i == 0:
                    op = P.add("sp", lambda e, o_=o_: e.dma_start(out=outd[0:112, :], in_=o_[16:128].ap), reads=[o_[:]], dma=f"o{i % 2}")
                else:
                    r0 = i * 128 - 16
                    op = P.add("sp", lambda e, o_=o_, r0=r0, n=n: e.dma_start(out=outd[r0:r0 + n, :], in_=o_[:n].ap), reads=[o_[:]], dma=f"o{i % 2}")
                outs.append(op)
            P.end_phase()
    return nc


_NC_CACHE = {}


def _host_layout(x, meta_tokens, lb_param, attn_norm_w, w_in, hgrn_norm_w, conv_w, w_out,
                 ffn_norm_w, w_up, ffn_conv_w, ffn_conv_b, w_down, final_norm_w):
    f = np.float32
    w_in = np.asarray(w_in, f)[0]
    wi = w_in.reshape(8, 128, 9, NH, 128)
    a1 = np.ascontiguousarray(wi[:, :, 0:3].transpose(3, 1, 0, 2, 4))
    a2 = np.ascontiguousarray(wi[:, :, 3:9].transpose(3, 1, 0, 2, 4))
    wo = np.ascontiguousarray(np.asarray(w_out, f)[0].reshape(8, 128, D).transpose(1, 0, 2))
    wu = np.asarray(w_up, f)[0].reshape(8, 128, 2, NFB, 128)
    wu = np.ascontiguousarray(wu.transpose(3, 1, 0, 2, 4))
    wd = np.ascontiguousarray(np.asarray(w_down, f)[0].reshape(NFB, 128, D).transpose(1, 0, 2))
    nrm = np.stack([np.asarray(attn_norm_w, f)[0], np.asarray(ffn_norm_w, f)[0], np.asarray(final_norm_w, f)], 0)
    nrm = np.ascontiguousarray(np.broadcast_to(nrm[None], (128, 3, D)))
    cvec = np.zeros((128, 256), f)
    lp = np.asarray(lb_param, f)
    cvec[:, 0:8] = lp[0].reshape(NH, 128).T
    cvec[:, 8:16] = lp[1].reshape(NH, 128).T
    cvec[:, 16] = np.asarray(hgrn_norm_w, f)[0]
    cw = np.asarray(conv_w, f)[0]
    cvec[:, 17:41] = cw.reshape(3, NH, 128).transpose(2, 1, 0).reshape(128, 24)
    fcw = np.asarray(ffn_conv_w, f)[0]
    cvec[:, 41:107] = fcw.reshape(3, NFB, 128).transpose(2, 1, 0).reshape(128, 66)
    cvec[:, 107:129] = np.asarray(ffn_conv_b, f)[0].reshape(NFB, 128).T
    cvec[:, 140] = EPS
    cmat = np.zeros((128, 3, 128), f)
    cmat[:, 0] = np.triu(np.ones((128, 128), f))
    cmat[:, 1] = np.eye(128, dtype=f)
    cmat[:, 2] = 1.0
    x = np.asarray(x, f)
    meta = np.asarray(meta_tokens, f)
    in_maps = []
    for c in range(NCORES):
        b, j = c // 4, c % 4
        xin = np.empty((TLOC, D), f)
        xin[:HALO] = meta if j == 0 else x[b, j * 2048 - HALO:j * 2048]
        xin[HALO:] = x[b, j * 2048:(j + 1) * 2048]
        cvc = cvec.copy()
        for r in range(NCORES):
            cvc[:, 129 + r] = 1.0 if (r // 4 == b and r < c) else 0.0
        in_maps.append({"xin": xin, "winA1": a1, "winA2": a2, "wout": wo, "wup": wu, "wdn": wd,
                        "nrmw": nrm, "cvec": cvc, "cmat": cmat})
    return in_maps


def kernel(**inputs):
    in_maps = _host_layout(**inputs)
    if "nc" not in _NC_CACHE:
        _NC_CACHE["nc"] = build_program()
    nc = _NC_CACHE["nc"]
    res = run_bass_kernel_spmd(nc, in_maps, core_ids=list(range(NCORES)))
    out = np.empty((2, 8192, D), np.float32)
    for c in range(NCORES):
        b, j = c // 4, c % 4
        out[b, j * 2048:(j + 1) * 2048] = res.results[c]["out"]
    return out
```
